# Optimizing a Trainium2 kernel written in Bass

```python
import jax, jax.numpy as jnp
from jax import lax
import numpy as np

D_MODEL = 1024
BATCH = 8
SEQ = 2048
DEPTH = 4

GRID_W = 64
CTX_LEN = 256
N_MIXERS = 2
GQA_HEADS = 16
GQA_KV_HEADS = 4
GQA_GROUP = GQA_HEADS // GQA_KV_HEADS
GQA_HEAD_DIM = D_MODEL // GQA_HEADS
MLA_HEADS = 16
MLA_Q_RANK = (3 * D_MODEL) // 8
MLA_KV_RANK = D_MODEL // 4
MLA_NOPE_DIM = 64
MLA_ROPE_DIM = 32
MLA_V_DIM = 64
FFN_HIDDEN = 4 * D_MODEL
ROPE_THETA = 10000.0
Q_BLOCK = 128
NORM_EPS = 1e-6
DEEPNORM_ALPHA = (2.0 * DEPTH) ** 0.25
DEEPNORM_BETA = (8.0 * DEPTH) ** -0.25

kernel_name = "hybrid_gqa_mla_deepnorm_dit"


def layer_norm(x, g, b):
    xf = x.astype(jnp.float32)
    mu = jnp.mean(xf, axis=-1, keepdims=True)
    var = jnp.mean(jnp.square(xf - mu), axis=-1, keepdims=True)
    return ((xf - mu) * lax.rsqrt(var + NORM_EPS) * g.astype(jnp.float32) + b.astype(jnp.float32)).astype(x.dtype)


def rms_norm(x, g):
    xf = x.astype(jnp.float32)
    ms = jnp.mean(jnp.square(xf), axis=-1, keepdims=True)
    return (xf * lax.rsqrt(ms + NORM_EPS) * g.astype(jnp.float32)).astype(x.dtype)


def grid_rope_tables(n_tok, rot_dim):
    rows = n_tok // GRID_W
    row = jnp.broadcast_to(jnp.arange(rows, dtype=jnp.float32)[:, None], (rows, GRID_W)).reshape(-1)
    col = jnp.broadcast_to(jnp.arange(GRID_W, dtype=jnp.float32)[None, :], (rows, GRID_W)).reshape(-1)
    axis_dim = rot_dim // 2
    inv_freq = ROPE_THETA ** (-jnp.arange(0, axis_dim, 2, dtype=jnp.float32) / axis_dim)
    ang_row = row[:, None] * inv_freq[None, :]
    ang_col = col[:, None] * inv_freq[None, :]
    return (jnp.cos(ang_row), jnp.sin(ang_row), jnp.cos(ang_col), jnp.sin(ang_col))


def rotate_axis(x, cos, sin):
    x1, x2 = jnp.split(x, 2, axis=-1)
    c = cos[:, None, :]
    s = sin[:, None, :]
    return jnp.concatenate([x1 * c - x2 * s, x2 * c + x1 * s], axis=-1)


def apply_grid_rope(x, tables):
    cr, sr, cc, sc = tables
    half = x.shape[-1] // 2
    xf = x.astype(jnp.float32)
    out = jnp.concatenate([rotate_axis(xf[..., :half], cr, sr), rotate_axis(xf[..., half:], cc, sc)], axis=-1)
    return out.astype(x.dtype)


def block_attention(q, k, v):
    b, sq, kvh, g, dk = q.shape
    nb = sq // Q_BLOCK
    scale = dk ** -0.5
    qb = jnp.moveaxis(q.reshape(b, nb, Q_BLOCK, kvh, g, dk), 1, 0)

    def one_block(qi):
        s = jnp.einsum('bqhgd,bkhd->bhgqk', qi, k, preferred_element_type=jnp.float32) * scale
        p = jax.nn.softmax(s, axis=-1).astype(v.dtype)
        return jnp.einsum('bhgqk,bkhe->bqhge', p, v)

    o = lax.map(one_block, qb)
    return jnp.moveaxis(o, 0, 1).reshape(b, sq, kvh * g * v.shape[-1])


def gqa_mixer(h_lat, h_ctx, w_qkv, q_norm, k_norm, w_o, rope, need_ctx):
    def project(h, rotate):
        b, n, _ = h.shape
        q, k, v = jnp.split(h @ w_qkv, [GQA_HEADS * GQA_HEAD_DIM, (GQA_HEADS + GQA_KV_HEADS) * GQA_HEAD_DIM], axis=-1)
        q = rms_norm(q.reshape(b, n, GQA_HEADS, GQA_HEAD_DIM), q_norm)
        k = rms_norm(k.reshape(b, n, GQA_KV_HEADS, GQA_HEAD_DIM), k_norm)
        v = v.reshape(b, n, GQA_KV_HEADS, GQA_HEAD_DIM)
        if rotate:
            q = apply_grid_rope(q, rope)
            k = apply_grid_rope(k, rope)
        return q.reshape(b, n, GQA_KV_HEADS, GQA_GROUP, GQA_HEAD_DIM), k, v

    q_lat, k_lat, v_lat = project(h_lat, True)
    q_ctx, k_ctx, v_ctx = project(h_ctx, False)
    k_all = jnp.concatenate([k_ctx, k_lat], axis=1)
    v_all = jnp.concatenate([v_ctx, v_lat], axis=1)
    y_lat = block_attention(q_lat, k_all, v_all) @ w_o
    y_ctx = block_attention(q_ctx, k_ctx, v_ctx) @ w_o if need_ctx else None
    return y_lat, y_ctx


def mla_mixer(h_lat, h_ctx, w_in, q_norm, kv_norm, w_uq, w_ukv, w_o, rope, need_ctx):
    def project(h, rotate):
        b, n, _ = h.shape
        cq, ckv, k_pe = jnp.split(h @ w_in, [MLA_Q_RANK, MLA_Q_RANK + MLA_KV_RANK], axis=-1)
        q = (rms_norm(cq, q_norm) @ w_uq).reshape(b, n, MLA_HEADS, MLA_NOPE_DIM + MLA_ROPE_DIM)
        q_nope, q_pe = jnp.split(q, [MLA_NOPE_DIM], axis=-1)
        kv = (rms_norm(ckv, kv_norm) @ w_ukv).reshape(b, n, MLA_HEADS, MLA_NOPE_DIM + MLA_V_DIM)
        k_nope, v = jnp.split(kv, [MLA_NOPE_DIM], axis=-1)
        k_pe = k_pe[:, :, None, :]
        if rotate:
            q_pe = apply_grid_rope(q_pe, rope)
            k_pe = apply_grid_rope(k_pe, rope)
        q = jnp.concatenate([q_nope, q_pe], axis=-1)[:, :, :, None, :]
        k = jnp.concatenate([k_nope, jnp.broadcast_to(k_pe, (b, n, MLA_HEADS, MLA_ROPE_DIM))], axis=-1)
        return q, k, v

    q_lat, k_lat, v_lat = project(h_lat, True)
    q_ctx, k_ctx, v_ctx = project(h_ctx, False)
    k_all = jnp.concatenate([k_ctx, k_lat], axis=1)
    v_all = jnp.concatenate([v_ctx, v_lat], axis=1)
    y_lat = block_attention(q_lat, k_all, v_all) @ w_o
    y_ctx = block_attention(q_ctx, k_ctx, v_ctx) @ w_o if need_ctx else None
    return y_lat, y_ctx


def sq_relu_mlp(h, w1, w2):
    return jnp.square(jax.nn.relu(h @ w1)) @ w2


def setup_inputs(seed: int = 0) -> dict:
    key = jax.random.key(seed)
    ks = jax.random.split(key, 20)
    d = D_MODEL
    n_a = (DEPTH + N_MIXERS - 1) // N_MIXERS
    n_b = DEPTH // N_MIXERS

    def w(k, shape, fan_in, gain=1.0):
        return jax.random.normal(k, shape, jnp.float32) * (gain * fan_in ** -0.5)

    def gain_vec(k, shape):
        return 1.0 + 0.02 * jax.random.normal(k, shape, jnp.float32)

    qkv_cols = (GQA_HEADS + 2 * GQA_KV_HEADS) * GQA_HEAD_DIM
    mla_in_cols = MLA_Q_RANK + MLA_KV_RANK + MLA_ROPE_DIM
    return {
        "x": jax.random.normal(ks[0], (BATCH, SEQ, d), jnp.float32),
        "c": jax.random.normal(ks[1], (BATCH, d), jnp.float32),
        "ctx": jax.random.normal(ks[2], (BATCH, CTX_LEN, d), jnp.float32),
        "c_ctx": jax.random.normal(ks[3], (d,), jnp.float32),
        "w_ada": w(ks[4], (DEPTH, d, 6 * d), d),
        "b_ada": 0.02 * jax.random.normal(ks[5], (DEPTH, 6 * d), jnp.float32),
        "ln_g": gain_vec(ks[6], (DEPTH, 2, d)),
        "ln_b": 0.02 * jax.random.normal(ks[7], (DEPTH, 2, d), jnp.float32),
        "mlp_w1": w(ks[8], (DEPTH, d, FFN_HIDDEN), d),
        "mlp_w2": w(ks[9], (DEPTH, FFN_HIDDEN, d), FFN_HIDDEN, DEEPNORM_BETA),
        "gqa_w_qkv": w(ks[10], (n_a, d, qkv_cols), d),
        "gqa_q_norm": gain_vec(ks[11], (n_a, GQA_HEAD_DIM)),
        "gqa_k_norm": gain_vec(ks[12], (n_a, GQA_HEAD_DIM)),
        "gqa_w_o": w(ks[13], (n_a, GQA_HEADS * GQA_HEAD_DIM, d), GQA_HEADS * GQA_HEAD_DIM, DEEPNORM_BETA),
        "mla_w_in": w(ks[14], (n_b, d, mla_in_cols), d),
        "mla_q_norm": gain_vec(ks[15], (n_b, MLA_Q_RANK)),
        "mla_kv_norm": gain_vec(ks[16], (n_b, MLA_KV_RANK)),
        "mla_w_uq": w(ks[17], (n_b, MLA_Q_RANK, MLA_HEADS * (MLA_NOPE_DIM + MLA_ROPE_DIM)), MLA_Q_RANK),
        "mla_w_ukv": w(ks[18], (n_b, MLA_KV_RANK, MLA_HEADS * (MLA_NOPE_DIM + MLA_V_DIM)), MLA_KV_RANK),
        "mla_w_o": w(ks[19], (n_b, MLA_HEADS * MLA_V_DIM, d), MLA_HEADS * MLA_V_DIM, DEEPNORM_BETA),
    }


def reference(x, c, ctx, c_ctx, w_ada, b_ada, ln_g, ln_b, mlp_w1, mlp_w2,
              gqa_w_qkv, gqa_q_norm, gqa_k_norm, gqa_w_o,
              mla_w_in, mla_q_norm, mla_kv_norm, mla_w_uq, mla_w_ukv, mla_w_o):
    n_tok = x.shape[1]
    rope_gqa = grid_rope_tables(n_tok, GQA_HEAD_DIM)
    rope_mla = grid_rope_tables(n_tok, MLA_ROPE_DIM)
    alpha = DEEPNORM_ALPHA
    silu_c = jax.nn.silu(c)
    silu_cc = jax.nn.silu(c_ctx)
    xc = ctx
    for i in range(DEPTH):
        need_ctx = i < DEPTH - 1
        mod_lat = (silu_c @ w_ada[i] + b_ada[i])[:, None, :]
        mod_ctx = silu_cc @ w_ada[i] + b_ada[i]
        sh_a, sc_a, g_a, sh_m, sc_m, g_m = jnp.split(mod_lat, 6, axis=-1)
        csh_a, csc_a, cg_a, csh_m, csc_m, cg_m = jnp.split(mod_ctx, 6, axis=-1)

        h_lat = x * (1.0 + sc_a) + sh_a
        h_ctx = xc * (1.0 + csc_a) + csh_a
        j = i // N_MIXERS
        if i % N_MIXERS == 0:
            y_lat, y_ctx = gqa_mixer(h_lat, h_ctx, gqa_w_qkv[j], gqa_q_norm[j], gqa_k_norm[j],
                                     gqa_w_o[j], rope_gqa, need_ctx)
        else:
            y_lat, y_ctx = mla_mixer(h_lat, h_ctx, mla_w_in[j], mla_q_norm[j], mla_kv_norm[j],
                                     mla_w_uq[j], mla_w_ukv[j], mla_w_o[j], rope_mla, need_ctx)

        x = layer_norm(alpha * x + g_a * y_lat, ln_g[i, 0], ln_b[i, 0])
        x = layer_norm(alpha * x + g_m * sq_relu_mlp(x * (1.0 + sc_m) + sh_m, mlp_w1[i], mlp_w2[i]),
                       ln_g[i, 1], ln_b[i, 1])
        if need_ctx:
            xc = layer_norm(alpha * xc + cg_a * y_ctx, ln_g[i, 0], ln_b[i, 0])
            xc = layer_norm(alpha * xc + cg_m * sq_relu_mlp(xc * (1.0 + csc_m) + csh_m, mlp_w1[i], mlp_w2[i]),
                            ln_g[i, 1], ln_b[i, 1])
    return x
```

```python
import numpy as np
from contextlib import ExitStack
import concourse.bass as bass
import concourse.mybir as mybir
from concourse.bass_utils import run_bass_kernel_spmd

F32 = mybir.dt.float32
BF16 = mybir.dt.bfloat16
AF = mybir.ActivationFunctionType
ALU = mybir.AluOpType

D = 1024
SEQ = 2048
CTX = 256
NT = SEQ + CTX
DEPTH = 4
ALPHA = (2.0 * DEPTH) ** 0.25
EPS = 1e-6
EPS_LN = EPS / (ALPHA * ALPHA)
GS = [512, 512, 512, 512, 256]
GO = [0, 512, 1024, 1536, 2048]
NSLOT = 10
FUSE_WAITS = False


class Chan:
    def __init__(self, name, sem, step):
        self.name, self.sem, self.step, self.count = name, sem, step, 0


class Buf:
    __slots__ = ("name", "w", "r", "excl")

    def __init__(self, name, excl=False):
        self.name, self.w, self.r, self.excl = name, None, {}, excl


class _Rec:
    def __init__(self):
        self.calls = []

    def __getattr__(self, name):
        def f(*a, **k):
            self.calls.append((name, a, k))
            return self
        return f


def _replay(calls):
    def fn(h, wait=None):
        ins = None
        for idx, (name, a, k) in enumerate(calls):
            ins = getattr(h, name)(*a, **k)
            if idx == 0 and wait is not None:
                ins._wait_ge(wait[0], wait[1])
        return ins
    return fn


class Prog:
    ENG = ("pe", "act", "dve", "pool", "sp")

    def __init__(self, nc, stack):
        self.nc, self.stack = nc, stack
        self.chan = {}
        for e in self.ENG:
            self.chan[e] = Chan(e, stack.enter_context(nc.semaphore("s_" + e)), 1)
        self.seen = {e: {} for e in self.ENG}
        self.ops = {e: [] for e in self.ENG}
        self.dchans = []

    def dma_chan(self, name):
        c = Chan(name, self.stack.enter_context(self.nc.semaphore("d_" + name)), 16)
        self.dchans.append(c)
        return c

    def _waits(self, eng, reads, writes):
        deps = {}

        def add(t):
            if t is not None and deps.get(t[0], 0) < t[1]:
                deps[t[0]] = t[1]
        me = self.chan[eng]
        for b in reads:
            add(b.w)
            if b.excl:
                for c, n in b.r.items():
                    if c is not me:
                        add((c, n))
        for b in writes:
            add(b.w)
            for c, n in b.r.items():
                add((c, n))
        out = []
        seen = self.seen[eng]
        for c, n in deps.items():
            if c is me and eng == "pe":
                continue
            if seen.get(c, 0) >= n:
                continue
            seen[c] = n
            out.append((c.sem, n))
        return out

    def op(self, eng, fn, reads=(), writes=()):
        waits = self._waits(eng, reads, writes)
        me = self.chan[eng]
        me.count += 1
        rec = _Rec()
        fn(rec)
        assert rec.calls
        self.ops[eng].append((waits, _replay(rec.calls), (me.sem, 1)))
        for b in reads:
            b.r[me] = me.count
        for b in writes:
            b.w = (me, me.count)
            b.r = {}

    def dma(self, eng, ch, out_ap, in_ap, reads=(), writes=()):
        waits = self._waits(eng, reads, writes)
        ch.count += 16
        self.ops[eng].append((waits, lambda h: h.dma_start(out=out_ap, in_=in_ap), (ch.sem, 16)))
        for b in reads:
            b.r[ch] = ch.count
        for b in writes:
            b.w = (ch, ch.count)
            b.r = {}

    def dma_multi(self, eng, ch, pairs, reads=(), writes=()):
        waits = self._waits(eng, reads, writes)
        for idx, (o, i) in enumerate(pairs):
            ch.count += 16
            self.ops[eng].append((waits if idx == 0 else [], (lambda h, o=o, i=i: h.dma_start(out=o, in_=i)),
                                  (ch.sem, 16)))
        for b in reads:
            b.r[ch] = ch.count
        for b in writes:
            b.w = (ch, ch.count)
            b.r = {}

    def final_wait(self, eng, chans):
        self.ops[eng].append(([(c.sem, c.count) for c in chans if c.count > 0], None, None))

    def emit(self):
        ops = self.ops

        def run(h, lst):
            for waits, fn, inc in lst:
                fused = None
                if fn is not None and waits and getattr(fn, "__name__", "") == "fn" and FUSE_WAITS:
                    fused = waits[-1]
                    waits = waits[:-1]
                for sem, n in waits:
                    h.wait_ge(sem, n)
                if fn is not None:
                    (fn(h, fused) if fused is not None else fn(h)).then_inc(inc[0], inc[1])
        with self.nc.Block() as block:
            @block.tensor
            def _(h):
                run(h, ops["pe"])

            @block.scalar
            def _(h):
                run(h, ops["act"])

            @block.vector
            def _(h):
                run(h, ops["dve"])

            @block.gpsimd
            def _(h):
                run(h, ops["pool"])

            @block.sync
            def _(h):
                run(h, ops["sp"])


def _rope_factors(rot_dim, nrep):
    axis_dim = rot_dim // 2
    inv_freq = (10000.0 ** (-np.arange(0, axis_dim, 2, dtype=np.float32) / np.float32(axis_dim))).astype(np.float32)
    nf = axis_dim // 2
    tab = np.zeros((rot_dim, 192), np.float32)
    rows = np.arange(32, dtype=np.float32)
    cols = np.arange(64, dtype=np.float32)
    for d in range(rot_dim):
        axis = d // axis_dim
        i = d % axis_dim
        f = i % nf
        sign = -1.0 if i < nf else 1.0
        if axis == 0:
            ang = (rows * inv_freq[f]).astype(np.float32)
            tab[d, 0:32] = np.cos(ang)
            tab[d, 32:64] = sign * np.sin(ang)
            tab[d, 64:128] = 1.0
            tab[d, 128:192] = 1.0
        else:
            ang = (cols * inv_freq[f]).astype(np.float32)
            tab[d, 0:32] = 1.0
            tab[d, 32:64] = 1.0
            tab[d, 64:128] = np.cos(ang)
            tab[d, 128:192] = sign * np.sin(ang)
    return np.tile(tab, (nrep, 1)).astype(np.float32)


def _swap_index(rot_dim):
    axis_dim = rot_dim // 2
    nf = axis_dim // 2
    sw = np.zeros(rot_dim, np.int64)
    for d in range(rot_dim):
        i = d % axis_dim
        sw[d] = d + nf if i < nf else d - nf
    return sw


def _perm_matrix(rot_dim, nrep):
    sw = _swap_index(rot_dim)
    n = rot_dim * nrep
    m = np.zeros((128, 128), np.float32)
    for p in range(n):
        base = (p // rot_dim) * rot_dim
        m[base + sw[p % rot_dim], p] = 1.0
    return m


STAGE = 99
SMALL = False


def build(nlayers=DEPTH):
    nc = bass.Bass("TRN2", target_bir_lowering=False)

    def din(name, shape):
        return nc.dram_tensor(name, list(shape), F32, kind="ExternalInput").ap()
    x_d = din("x", [SEQ, D])
    ctx_d = din("ctx", [CTX, D])
    cc_d = din("cc", [128, 16])
    wada_d = din("w_ada", [1, 1, 8] if (SMALL and STAGE < 0.7) else [1 if SMALL else DEPTH, D, 6 * D])
    bada_d = din("bada", [128, DEPTH * 48])
    lng_d = din("lng", [128, DEPTH * 16])
    lnb_d = din("lnb", [128, DEPTH * 16])
    w1_d = din("w1", [1, 1, 8] if SMALL else [DEPTH, D, 4 * D])
    w2_d = din("w2", [1, 1, 8] if SMALL else [DEPTH, 4 * D, D])
    wqkv_d = din("wqkv", [1, 1, 8] if SMALL else [2, D, 1536])
    wog_d = din("wo_g", [1, 1, 8] if SMALL else [2, D, D])
    gn_d = din("gn", [128, 8])
    win_d = din("win", [1, 1, 8] if SMALL else [2, D, 672])
    mn_d = din("mn", [128, 10])
    wuq_d = din("wuq", [1, 1, 8] if SMALL else [2, 384, 1536])
    wukv_d = din("wukv", [1, 1, 8] if SMALL else [2, 256, 2048])
    wom_d = din("wo_m", [1, 1, 8] if SMALL else [2, D, D])
    cm_d = din("cmats", [128, 5 * 128])
    rg_d = din("ropeg", [128, 192])
    rm_d = din("ropem", [128, 192])
    out_d = nc.dram_tensor("out", [SEQ, D], F32, kind="ExternalOutput").ap()

    with ExitStack() as st:
        P = Prog(nc, st)

        def sb(name, shape, dt):
            return st.enter_context(nc.sbuf_tensor(name, list(shape), dt))

        XT = sb("XT", [128, 8, NT], F32)
        HT = sb("HT", [128, 8 * NT], BF16)
        AT = sb("AT", [128, 8, NT], BF16)
        WS = sb("WS", [128, NSLOT, 1024], BF16)
        QH = sb("QH", [128, NT], BF16)
        KH = sb("KH", [128, NT], BF16)
        VH = sb("VH", [128, 18, 128], BF16)
        QPE = sb("QPE", [128, NT], BF16)
        UT = sb("UT", [128, 4, 512], BF16)
        RL = sb("RL", [128, 2, 512], BF16)
        PT = sb("PT", [128, 3, 512], BF16)
        SC = sb("SC", [128, 5, 512], F32)
        MOD = sb("MOD", [128, 2, 48, 2], F32)
        CM = sb("CM", [128, 5, 128], F32)
        RG = sb("RG", [128, 192], F32)
        RM = sb("RM", [128, 192], F32)
        CC = sb("CC", [128, 16], F32)
        SIL = sb("SIL", [128, 8, 2], BF16)
        BADA = sb("BADA", [128, DEPTH, 48], F32)
        LNG = sb("LNG", [128, DEPTH, 2, 8], F32)
        LNB = sb("LNB", [128, DEPTH, 2, 8], F32)
        GN = sb("GN", [128, 2, 4], F32)
        MN = sb("MN", [128, 2, 5], F32)
        ONEB = sb("ONEB", [128, 64], BF16)
        PS = [st.enter_context(nc.psum_tensor("ps%d" % i, [128, 512], F32)) for i in range(8)]

        IDENT, ONES, BONES, PERMG, PERMM = (CM[:, i, :] for i in range(5))

        bXT = [[Buf("xt%d_%d" % (k, g)) for g in range(5)] for k in range(8)]
        bHB = [Buf("hb%d" % g) for g in range(5)]
        bAT = [[[Buf("at%d_%d_%d" % (j, h, g)) for g in range(5)] for h in range(2)] for j in range(8)]
        bWS = [Buf("ws%d" % i) for i in range(NSLOT)]
        cWS = [P.dma_chan("ws%d" % i) for i in range(NSLOT)]
        bPS = [Buf("ps%d" % i, excl=True) for i in range(8)]
        bSC = [Buf("sc%d" % i) for i in range(5)]
        bPT = [Buf("pt%d" % i) for i in range(3)]
        bRL = [Buf("rl%d" % i) for i in range(2)]
        bUT = Buf("ut")
        bQH, bKH, bVH, bQPE = Buf("qh"), Buf("kh"), Buf("vh"), Buf("qpe")
        bMOD = [Buf("mod0"), Buf("mod1")]
        bCONST = Buf("const")
        cIN = P.dma_chan("in")
        cX = [P.dma_chan("x0"), P.dma_chan("x1")]
        cOUT = [P.dma_chan("out0"), P.dma_chan("out1")]

        free_banks = list(range(8))

        def ps_alloc():
            assert free_banks, "out of PSUM banks"
            return free_banks.pop(0)

        def ps_free(b):
            free_banks.append(b)

        sc_rr = [0]

        def sc_next(exclude=()):
            while True:
                i = sc_rr[0] % 5
                sc_rr[0] += 1
                if i not in exclude:
                    return i

        ws_rr = [0]

        def slot_next():
            i = ws_rr[0] % NSLOT
            ws_rr[0] += 1
            return i

        def wload(src, view3=None):
            i = slot_next()
            dst = WS[:, i, :]
            if view3 is not None:
                dst = dst.rearrange("p (k c) -> p k c", k=view3)
            if len(src.shape) == 2 and src.shape[1] < 1024:
                dst = WS[:, i, 0:src.shape[1]]
            P.dma("pool", cWS[i], dst, src, writes=[bWS[i]])
            return i

        def wk(i, k, c0=0, c1=128, kc=128):
            return WS[:, i, k * kc + c0:k * kc + c1]

        def xt(k, g, r0=0, r1=128):
            return XT[r0:r1, k, GO[g]:GO[g] + GS[g]]

        def hb(g):
            return GO[g] * 8

        def ht(g, k, t0=0, t1=None):
            n = GS[g]
            t1 = n if t1 is None else t1
            return HT[:, hb(g) + k * n + t0:hb(g) + k * n + t1]

        def halias(g, off, n, r0=0, r1=128):
            return HT[r0:r1, hb(g) + off:hb(g) + off + n]

        def mm(bank, pairs, reads, ncols, r0=0, r1=128, c0=0):
            out = PS[bank][r0:r1, c0:c0 + ncols]
            n = len(pairs)

            def fn(h):
                ins = None
                for i, (l, r) in enumerate(pairs):
                    ins = h.matmul(out, lhsT=l, rhs=r, start=(i == 0), stop=(i == n - 1))
                return ins
            P.op("pe", fn, reads=reads, writes=[bPS[bank]])

        P.dma("sp", cIN, CM[:].rearrange("p a b -> p (a b)"), cm_d, writes=[bCONST])
        P.dma("sp", cIN, RG[:], rg_d, writes=[bCONST])
        P.dma("sp", cIN, RM[:], rm_d, writes=[bCONST])
        P.dma("sp", cIN, CC[:], cc_d, writes=[bCONST])
        P.dma("sp", cIN, BADA[:].rearrange("p a b -> p (a b)"), bada_d, writes=[bCONST])
        P.dma("sp", cIN, LNG[:].rearrange("p a b c -> p (a b c)"), lng_d, writes=[bCONST])
        P.dma("sp", cIN, LNB[:].rearrange("p a b c -> p (a b c)"), lnb_d, writes=[bCONST])
        P.dma("sp", cIN, GN[:].rearrange("p a b -> p (a b)"), gn_d, writes=[bCONST])
        P.dma("sp", cIN, MN[:].rearrange("p a b -> p (a b)"), mn_d, writes=[bCONST])
        bSIL = Buf("sil")
        P.op("act", lambda h: h.activation(out=SIL[:].rearrange("p a b -> p (a b)"), in_=CC[:], func=AF.Silu),
             reads=[bCONST], writes=[bSIL])
        bONEB = Buf("oneb")
        P.op("dve", lambda h: h.memset(ONEB[:], 1.0), writes=[bONEB])

        def emit_mod(l, c_lo, c_hi):
            par = l % 2
            for c in range(c_lo, c_hi):
                s = wload(wada_d[l, :, c * 128:(c + 1) * 128].rearrange("(k p) c -> p k c", p=128), view3=8)
                b = ps_alloc()
                mm(b, [(wk(s, k), SIL[:, k, :]) for k in range(8)], [bWS[s], bSIL], 2)
                col = BADA[:, l, c:c + 1]
                P.op("dve", lambda h, b=b, c=c, col=col: h.tensor_scalar(
                    out=MOD[:, par, c, :], in0=PS[b][:, 0:2], scalar1=col, scalar2=None, op0=ALU.add),
                    reads=[bPS[b], bCONST], writes=[bMOD[par]])
                ps_free(b)
            if c_hi == 48:
                for lo, opn, val in ((8, ALU.add, 1.0), (16, ALU.mult, 1.0 / ALPHA), (32, ALU.add, 1.0),
                                     (40, ALU.mult, 1.0 / ALPHA)):
                    P.op("dve", lambda h, lo=lo, opn=opn, val=val: h.tensor_scalar(
                        out=MOD[:, par, lo:lo + 8, :], in0=MOD[:, par, lo:lo + 8, :], scalar1=val, scalar2=None,
                        op0=opn), reads=[bMOD[par]], writes=[bMOD[par]])

        def modc(l, part, k, g):
            return MOD[:, l % 2, part * 8 + k, (1 if g == 4 else 0):(2 if g == 4 else 1)]

        def emit_load():
            for t in range(18):
                src = x_d[t * 128:(t + 1) * 128, :] if t < 16 else ctx_d[(t - 16) * 128:(t - 15) * 128, :]
                g = min(t // 4, 4)
                par = t % 2
                xin = HT[:, par * 4096:par * 4096 + 2048].bitcast(F32)
                bx = bHB[par]
                P.dma("sp", cX[par], xin, src, writes=[bx])
                for half in range(2):
                    b = ps_alloc()

                    def fn(h, b=b, half=half, xin=xin):
                        ins = None
                        for q in range(4):
                            k = half * 4 + q
                            ins = h.matmul(PS[b][:, q * 128:(q + 1) * 128], lhsT=xin[:, k * 128:(k + 1) * 128], rhs=IDENT,
                                           start=True, stop=True)
                        return ins
                    P.op("pe", fn, reads=[bx, bCONST], writes=[bPS[b]])
                    tl = t * 128
                    P.op("dve", lambda h, b=b, half=half, tl=tl: h.tensor_copy(
                        out=XT[:, half * 4:half * 4 + 4, tl:tl + 128],
                        in_=PS[b][:, :].rearrange("p (k c) -> p k c", k=4)),
                        reads=[bPS[b]], writes=[bXT[k][g] for k in range(half * 4, half * 4 + 4)])
                    ps_free(b)

        def emit_h(l, g, part_sh, part_sc):
            for k in range(8):
                P.op("act", lambda h, k=k: h.activation(
                    out=ht(g, k), in_=xt(k, g), func=AF.Identity,
                    bias=modc(l, part_sh, k, g), scale=modc(l, part_sc, k, g)),
                    reads=[bXT[k][g], bMOD[l % 2]], writes=[bHB[g]])

        def emit_rstd(bank, scale, eps, n, r0=0, r1=128):
            i = sc_next()
            P.op("act", lambda h: h.activation(out=SC[r0:r1, i, 0:n], in_=PS[bank][r0:r1, 0:n], func=AF.Ln,
                                               bias=eps, scale=scale), reads=[bPS[bank]], writes=[bSC[i]])
            P.op("act", lambda h: h.activation(out=SC[r0:r1, i, 0:n], in_=SC[r0:r1, i, 0:n], func=AF.Exp,
                                               scale=-0.5), reads=[bSC[i]], writes=[bSC[i]])
            return i

        def emit_ln(l, which, g):
            n = GS[g]
            bm = ps_alloc()
            be = ps_alloc()
            mm(bm, [(ONES, xt(k, g)) for k in range(8)], [bCONST] + [bXT[k][g] for k in range(8)], n)
            sq = []
            for k in range(8):
                i = sc_next()
                P.op("act", lambda h, k=k, i=i: h.activation(out=SC[:, i, 0:n], in_=xt(k, g), func=AF.Square),
                     reads=[bXT[k][g]], writes=[bSC[i]])
                P.op("pe", lambda h, k=k, i=i: h.matmul(PS[be][:, 0:n], lhsT=ONES, rhs=SC[:, i, 0:n],
                                                        start=(k == 0), stop=(k == 7)),
                     reads=[bCONST, bSC[i]], writes=[bPS[be]])
            im = sc_next()
            P.op("act", lambda h: h.activation(out=SC[:, im, 0:n], in_=PS[bm][:, 0:n], func=AF.Copy, scale=1.0 / D),
                 reads=[bPS[bm]], writes=[bSC[im]])
            iq = sc_next((im,))
            P.op("dve", lambda h: h.tensor_tensor(out=SC[:, iq, 0:n], in0=SC[:, im, 0:n], in1=SC[:, im, 0:n],
                                                  op=ALU.mult), reads=[bSC[im]], writes=[bSC[iq]])
            P.op("dve", lambda h: h.scalar_tensor_tensor(out=SC[:, iq, 0:n], in0=PS[be][:, 0:n], scalar=1.0 / D,
                                                         in1=SC[:, iq, 0:n], op0=ALU.mult, op1=ALU.subtract),
                 reads=[bPS[be], bSC[iq]], writes=[bSC[iq]])
            P.op("act", lambda h: h.activation(out=SC[:, iq, 0:n], in_=SC[:, iq, 0:n], func=AF.Ln, bias=EPS_LN,
                                               scale=1.0), reads=[bSC[iq]], writes=[bSC[iq]])
            P.op("act", lambda h: h.activation(out=SC[:, iq, 0:n], in_=SC[:, iq, 0:n], func=AF.Exp, scale=-0.5),
                 reads=[bSC[iq]], writes=[bSC[iq]])
            ps_free(bm)
            ps_free(be)
            for k in range(8):
                i = sc_next((im, iq))
                P.op("dve", lambda h, k=k, i=i: h.tensor_tensor(out=SC[:, i, 0:n], in0=xt(k, g), in1=SC[:, im, 0:n],
                                                                op=ALU.subtract),
                     reads=[bXT[k][g], bSC[im]], writes=[bSC[i]])
                P.op("dve", lambda h, k=k, i=i: h.tensor_tensor(out=SC[:, i, 0:n], in0=SC[:, i, 0:n],
                                                                in1=SC[:, iq, 0:n], op=ALU.mult),
                     reads=[bSC[i], bSC[iq]], writes=[bSC[i]])
                P.op("act", lambda h, k=k, i=i: h.activation(out=xt(k, g), in_=SC[:, i, 0:n], func=AF.Identity,
                                                             bias=LNB[:, l, which, k:k + 1],
                                                             scale=LNG[:, l, which, k:k + 1]),
                     reads=[bSC[i], bCONST], writes=[bXT[k][g]])

        def emit_resid(l, part, oc, g, bank):
            n = GS[g]
            P.op("dve", lambda h: h.scalar_tensor_tensor(out=xt(oc, g), in0=PS[bank][:, 0:n],
                                                         scalar=modc(l, part, oc, g), in1=xt(oc, g),
                                                         op0=ALU.mult, op1=ALU.add),
                 reads=[bPS[bank], bMOD[l % 2], bXT[oc][g]], writes=[bXT[oc][g]])

        def emit_mlp(l, groups, hook=None):
            for e in range(8):
                s1 = []
                s2 = []
                for q in range(4):
                    hc = e * 4 + q
                    s1.append(wload(w1_d[l, :, hc * 128:(hc + 1) * 128].rearrange("(k p) c -> p k c", p=128), view3=8))
                for q in range(4):
                    hc = e * 4 + q
                    s2.append(wload(w2_d[l, hc * 128:(hc + 1) * 128, :]))
                for g in groups:
                    n = GS[g]
                    for q in range(4):
                        b = ps_alloc()
                        mm(b, [(wk(s1[q], k), ht(g, k)) for k in range(8)], [bWS[s1[q]], bHB[g]], n)
                        r = q % 2
                        P.op("act", lambda h, b=b, r=r: h.activation(out=RL[:, r, 0:n], in_=PS[b][:, 0:n],
                                                                    func=AF.Relu),
                             reads=[bPS[b]], writes=[bRL[r]])
                        ps_free(b)
                        P.op("dve", lambda h, q=q, r=r: h.tensor_tensor(out=UT[:, q, 0:n], in0=RL[:, r, 0:n],
                                                                       in1=RL[:, r, 0:n], op=ALU.mult),
                             reads=[bRL[r]], writes=[bUT])
                    for oc in range(8):
                        b = ps_alloc()
                        mm(b, [(WS[:, s2[q], oc * 128:(oc + 1) * 128], UT[:, q, 0:n]) for q in range(4)],
                           [bWS[s] for s in s2] + [bUT], n)
                        emit_resid(l, 5, oc, g, b)
                        ps_free(b)
                if hook is not None:
                    hook(e)

        def emit_rope(bank, g, out_ap, out_bufs, RT, perm, rows, gcol, gswcol, norm_scale):
            n = GS[g]
            r0, r1 = rows
            nr = r1 - r0
            ist = None
            if norm_scale is not None:
                isq = sc_next()
                P.op("act", lambda h: h.activation(out=SC[r0:r1, isq, 0:n], in_=PS[bank][r0:r1, 0:n], func=AF.Square),
                     reads=[bPS[bank]], writes=[bSC[isq]])
                bs = ps_alloc()
                mm(bs, [(BONES[r0:r1, r0:r1], SC[r0:r1, isq, 0:n])], [bCONST, bSC[isq]], n, r0, r1)
                ist = emit_rstd(bs, norm_scale, EPS, n, r0, r1)
                ps_free(bs)
            if g == 4:
                if ist is None:
                    P.op("act", lambda h: h.copy(out=out_ap, in_=PS[bank][r0:r1, 0:n]), reads=[bPS[bank]],
                         writes=out_bufs)
                else:
                    P.op("dve", lambda h: h.scalar_tensor_tensor(out=out_ap, in0=PS[bank][r0:r1, 0:n], scalar=gcol,
                                                                 in1=SC[r0:r1, ist, 0:n], op0=ALU.mult, op1=ALU.mult),
                         reads=[bPS[bank], bSC[ist], bCONST], writes=out_bufs)
                return
            ix = sc_next()
            P.op("act", lambda h: h.copy(out=SC[r0:r1, ix, 0:n], in_=PS[bank][r0:r1, 0:n]), reads=[bPS[bank]],
                 writes=[bSC[ix]])
            bw = ps_alloc()
            mm(bw, [(perm[r0:r1, r0:r1], SC[r0:r1, ix, 0:n])], [bCONST, bSC[ix]], n, r0, r1)
            rr0 = g * 8
            Ac = RT[r0:r1, rr0:rr0 + 8].unsqueeze(2).to_broadcast([nr, 8, 64])
            As = RT[r0:r1, 32 + rr0:32 + rr0 + 8].unsqueeze(2).to_broadcast([nr, 8, 64])
            Bc = RT[r0:r1, 64:128].unsqueeze(1).to_broadcast([nr, 8, 64])
            Bs = RT[r0:r1, 128:192].unsqueeze(1).to_broadcast([nr, 8, 64])

            def v3(ap):
                return ap.rearrange("p (r c) -> p r c", c=64)
            i1 = sc_next()
            i2 = sc_next()
            g1 = gcol if gcol is not None else 1.0
            g2 = gswcol if gswcol is not None else 1.0
            P.op("dve", lambda h: h.scalar_tensor_tensor(out=v3(SC[r0:r1, i1, 0:n]), in0=v3(PS[bank][r0:r1, 0:n]),
                                                         scalar=g1, in1=Ac, op0=ALU.mult, op1=ALU.mult),
                 reads=[bPS[bank], bCONST], writes=[bSC[i1]])
            P.op("dve", lambda h: h.tensor_tensor(out=v3(SC[r0:r1, i1, 0:n]), in0=v3(SC[r0:r1, i1, 0:n]), in1=Bc,
                                                  op=ALU.mult), reads=[bSC[i1], bCONST], writes=[bSC[i1]])
            P.op("dve", lambda h: h.scalar_tensor_tensor(out=v3(SC[r0:r1, i2, 0:n]), in0=v3(PS[bw][r0:r1, 0:n]),
                                                         scalar=g2, in1=As, op0=ALU.mult, op1=ALU.mult),
                 reads=[bPS[bw], bCONST], writes=[bSC[i2]])
            ps_free(bw)
            P.op("dve", lambda h: h.tensor_tensor(out=v3(SC[r0:r1, i2, 0:n]), in0=v3(SC[r0:r1, i2, 0:n]), in1=Bs,
                                                  op=ALU.mult), reads=[bSC[i2], bCONST], writes=[bSC[i2]])
            if ist is None:
                P.op("dve", lambda h: h.tensor_tensor(out=out_ap, in0=SC[r0:r1, i1, 0:n], in1=SC[r0:r1, i2, 0:n],
                                                      op=ALU.add), reads=[bSC[i1], bSC[i2]], writes=out_bufs)
            else:
                P.op("dve", lambda h: h.tensor_tensor(out=SC[r0:r1, i1, 0:n], in0=SC[r0:r1, i1, 0:n],
                                                      in1=SC[r0:r1, i2, 0:n], op=ALU.add),
                     reads=[bSC[i1], bSC[i2]], writes=[bSC[i1]])
                P.op("dve", lambda h: h.tensor_tensor(out=out_ap, in0=SC[r0:r1, i1, 0:n], in1=SC[r0:r1, ist, 0:n],
                                                      op=ALU.mult), reads=[bSC[i1], bSC[ist]], writes=out_bufs)

        def emit_attn_head(q_ap, q_bufs, k_ap, v_ap, kv_bufs, scale, j, orow, qgroups):
            for g in qgroups:
                n = GS[g]
                tiles = list(range(18)) if g < 4 else [16, 17]
                acc = ps_alloc()
                LA = 2
                nti = len(tiles)

                def issue_s(idx):
                    t = tiles[idx]
                    sbk = ps_alloc()
                    mm(sbk, [(k_ap(t), q_ap(g))], kv_bufs(t) + q_bufs(g), n)
                    pi = idx % 3
                    P.op("act", lambda h: h.activation(out=PT[:, pi, 0:n], in_=PS[sbk][:, 0:n], func=AF.Exp,
                                                       scale=scale), reads=[bPS[sbk]], writes=[bPT[pi]])
                    ps_free(sbk)
                for idx in range(min(LA, nti)):
                    issue_s(idx)
                for idx in range(nti):
                    if idx + LA < nti:
                        issue_s(idx + LA)
                    t = tiles[idx]
                    pi = idx % 3
                    P.op("pe", lambda h: h.matmul(PS[acc][:, 0:n], lhsT=v_ap(t), rhs=PT[:, pi, 0:n],
                                                  start=(idx == 0), stop=(idx == nti - 1)),
                         reads=kv_bufs(t) + [bPT[pi]], writes=[bPS[acc]])
                lrow = 64 - orow
                ir = sc_next()
                P.op("dve", lambda h: h.reciprocal(out=SC[orow:orow + 64, ir, 0:n], in_=PS[acc][lrow:lrow + 64, 0:n]),
                     reads=[bPS[acc]], writes=[bSC[ir]])
                P.op("dve", lambda h: h.tensor_tensor(out=AT[orow:orow + 64, j, GO[g]:GO[g] + n],
                                                      in0=PS[acc][orow:orow + 64, 0:n],
                                                      in1=SC[orow:orow + 64, ir, 0:n], op=ALU.mult),
                     reads=[bPS[acc], bSC[ir]], writes=[bAT[j][orow // 64][g]])
                ps_free(acc)

        GQA_PAIRS = [(8 * (j // 4) + (j % 4), 8 * (j // 4) + 4 + (j % 4)) for j in range(8)]

        def k_alias(g, c, r0, r1, t0, t1):
            n = GS[g]
            return HT[r0:r1, hb(g) + c * n + t0:hb(g) + c * n + t1]

        def v_alias(g, tl, kv):
            n = GS[g]
            o = hb(g) + 2 * n + (tl * 4 + kv) * 128
            return HT[:, o:o + 128]

        def emit_gqa(l, jl, need_ctx):
            wq = wqkv_d[jl]
            gq, gqs, gk, gks = (GN[:, jl, i:i + 1] for i in range(4))
            qgroups = list(range(5)) if need_ctx else list(range(4))
            for j in range(8):
                i = slot_next()
                dst = WS[:, i, :].rearrange("p (k c) -> p k c", k=8)
                P.dma_multi("pool", cWS[i], [
                    (dst[:, :, hh * 64:(hh + 1) * 64],
                     wq[:, GQA_PAIRS[j][hh] * 64:(GQA_PAIRS[j][hh] + 1) * 64].rearrange("(k p) c -> p k c", p=128))
                    for hh in range(2)], writes=[bWS[i]])
                for g in qgroups:
                    n = GS[g]
                    b = ps_alloc()
                    mm(b, [(wk(i, k), ht(g, k)) for k in range(8)], [bWS[i], bHB[g]], n)
                    emit_rope(b, g, AT[:, j, GO[g]:GO[g] + n], [bAT[j][0][g], bAT[j][1][g]], RG, PERMG, (0, 128),
                              gq, gqs, 1.0 / 64)
                    ps_free(b)
            sk_ = [wload(wq[:, 1024 + c * 128:1024 + (c + 1) * 128].rearrange("(k p) c -> p k c", p=128), view3=8)
                   for c in range(2)]
            sv_ = [wload(wq[:, 1280 + c * 128:1280 + (c + 1) * 128].rearrange("(k p) c -> p k c", p=128), view3=8)
                   for c in range(2)]
            for g in range(5):
                n = GS[g]
                nt = n // 128
                kb = [ps_alloc(), ps_alloc()]
                for c in range(2):
                    mm(kb[c], [(wk(sk_[c], k), ht(g, k)) for k in range(8)], [bWS[sk_[c]], bHB[g]], n)
                vb = [ps_alloc() for _ in range((nt + 1) // 2)]
                for tl in range(nt):
                    for c in range(2):
                        mm(vb[tl // 2], [(ht(g, k, tl * 128, (tl + 1) * 128), wk(sv_[c], k)) for k in range(8)],
                           [bWS[sv_[c]], bHB[g]], 128, c0=(tl % 2) * 256 + c * 128)
                for c in range(2):
                    emit_rope(kb[c], g, k_alias(g, c, 0, 128, 0, n), [bHB[g]], RG, PERMG, (0, 128), gk, gks, 1.0 / 64)
                    ps_free(kb[c])
                for tl in range(nt):
                    for kv in range(4):
                        src = PS[vb[tl // 2]][:, (tl % 2) * 256 + kv * 64:(tl % 2) * 256 + (kv + 1) * 64]
                        va = v_alias(g, tl, kv)
                        vcols = (0, 64) if kv % 2 == 0 else (64, 128)
                        ocols = (64, 128) if kv % 2 == 0 else (0, 64)
                        P.op("act", lambda h, src=src, va=va, vcols=vcols: h.copy(out=va[:, vcols[0]:vcols[1]], in_=src),
                             reads=[bPS[vb[tl // 2]]], writes=[bHB[g]])
                        P.op("dve", lambda h, va=va, ocols=ocols: h.tensor_copy(out=va[:, ocols[0]:ocols[1]],
                                                                              in_=ONEB[:, :]),
                             reads=[bONEB], writes=[bHB[g]])
                for b in vb:
                    ps_free(b)
            for j in range(8):
                c = j // 4
                for hh in range(2):
                    r0 = hh * 64

                    def q_ap(g, j=j, r0=r0):
                        return AT[r0:r0 + 64, j, GO[g]:GO[g] + GS[g]]

                    def k_ap(t, c=c, r0=r0):
                        gk_ = min(t // 4, 4)
                        tl = t - 4 * gk_
                        return k_alias(gk_, c, r0, r0 + 64, tl * 128, (tl + 1) * 128)

                    def v_ap(t, c=c, hh=hh):
                        gk_ = min(t // 4, 4)
                        return v_alias(gk_, t - 4 * gk_, 2 * c + hh)
                    emit_attn_head(q_ap, lambda g, j=j, hh=hh: [bAT[j][hh][g]], k_ap, v_ap,
                                   lambda t: [bHB[min(t // 4, 4)]], 0.125, j, r0,
                                   list(range(5)) if need_ctx else list(range(4)))

        def emit_oproj(l, wo, groups, gqa):
            for oc in range(8):
                i = slot_next()
                dst = WS[:, i, :].rearrange("p (j c) -> p j c", j=8)
                if gqa:
                    P.dma_multi("pool", cWS[i], [
                        (dst[hh * 64:(hh + 1) * 64, 4 * c:4 * c + 4, :],
                         wo[(8 * c + 4 * hh) * 64:(8 * c + 4 * hh + 4) * 64, oc * 128:(oc + 1) * 128].rearrange(
                             "(i p) c -> p i c", p=64))
                        for c in range(2) for hh in range(2)], writes=[bWS[i]])
                else:
                    P.dma("pool", cWS[i], dst, wo[:, oc * 128:(oc + 1) * 128].rearrange("(j p) c -> p j c", p=128),
                          writes=[bWS[i]])
                for g in groups:
                    n = GS[g]
                    b = ps_alloc()
                    mm(b, [(wk(i, j), AT[:, j, GO[g]:GO[g] + n]) for j in range(8)],
                       [bWS[i]] + [bAT[j][hh][g] for j in range(8) for hh in range(2)], n)
                    emit_resid(l, 2, oc, g, b)
                    ps_free(b)

        def cq_alias(g, k, t0=0, t1=None):
            n = GS[g]
            t1 = n if t1 is None else t1
            return HT[:, hb(g) + k * n + t0:hb(g) + k * n + t1]

        def ckv_alias(g, k, t0=0, t1=None):
            n = GS[g]
            t1 = n if t1 is None else t1
            return HT[:, hb(g) + (3 + k) * n + t0:hb(g) + (3 + k) * n + t1]

        def kpe_alias(g):
            n = GS[g]
            return HT[0:32, hb(g) + 5 * n:hb(g) + 6 * n]

        def emit_mla(l, jl, need_ctx):
            win = win_d[jl]
            qgroups = list(range(5)) if need_ctx else list(range(4))
            s_in = [wload(win[:, c * 128:(c + 1) * 128].rearrange("(k p) c -> p k c", p=128), view3=8) for c in range(5)]
            i_pe = slot_next()
            P.dma("pool", cWS[i_pe], WS[:, i_pe, :].rearrange("p (k c) -> p k c", k=8)[:, :, 0:32],
                  win[:, 640:672].rearrange("(k p) c -> p k c", p=128), writes=[bWS[i_pe]])
            for g in range(5):
                n = GS[g]
                cb = [ps_alloc() for _ in range(5)]
                for c in range(5):
                    mm(cb[c], [(wk(s_in[c], k), ht(g, k)) for k in range(8)], [bWS[s_in[c]], bHB[g]], n)
                pb_ = ps_alloc()
                mm(pb_, [(wk(i_pe, k, 0, 32), ht(g, k)) for k in range(8)], [bWS[i_pe], bHB[g]], n, 0, 32)
                for (lo, hi, scale, ncol0, al) in ((0, 3, 1.0 / 384, 0, cq_alias), (3, 5, 1.0 / 256, 3, ckv_alias)):
                    ms = ps_alloc()
                    for c in range(lo, hi):
                        i = sc_next()
                        P.op("act", lambda h, c=c, i=i: h.activation(out=SC[:, i, 0:n], in_=PS[cb[c]][:, 0:n],
                                                                     func=AF.Square),
                             reads=[bPS[cb[c]]], writes=[bSC[i]])
                        P.op("pe", lambda h, c=c, i=i, lo=lo, hi=hi, ms=ms: h.matmul(
                            PS[ms][:, 0:n], lhsT=ONES, rhs=SC[:, i, 0:n], start=(c == lo), stop=(c == hi - 1)),
                            reads=[bCONST, bSC[i]], writes=[bPS[ms]])
                    ist = emit_rstd(ms, scale, EPS, n)
                    ps_free(ms)
                    for c in range(lo, hi):
                        P.op("dve", lambda h, c=c, lo=lo, al=al, ist=ist: h.scalar_tensor_tensor(
                            out=al(g, c - lo), in0=PS[cb[c]][:, 0:n], scalar=MN[:, jl, c:c + 1],
                            in1=SC[:, ist, 0:n], op0=ALU.mult, op1=ALU.mult),
                            reads=[bPS[cb[c]], bSC[ist], bCONST], writes=[bHB[g]])
                        ps_free(cb[c])
                emit_rope(pb_, g, kpe_alias(g), [bHB[g]], RM, PERMM, (0, 32), None, None, None)
                ps_free(pb_)
            wuq = wuq_d[jl]
            wukv = wukv_d[jl]
            for h4 in range(4):
                i_q = slot_next()
                dstq = WS[:, i_q, :].rearrange("p (k c) -> p k c", k=8)
                P.dma_multi("pool", cWS[i_q], [
                    (dstq[:, 0:3, hh * 32:(hh + 1) * 32],
                     wuq[:, (4 * h4 + hh) * 96 + 64:(4 * h4 + hh) * 96 + 96].rearrange("(k p) c -> p k c", p=128))
                    for hh in range(4)], writes=[bWS[i_q]])
                for g in qgroups:
                    n = GS[g]
                    b = ps_alloc()
                    mm(b, [(wk(i_q, k), cq_alias(g, k)) for k in range(3)], [bWS[i_q], bHB[g]], n)
                    emit_rope(b, g, QPE[:, GO[g]:GO[g] + n], [bQPE], RM, PERMM, (0, 128), None, None, None)
                    ps_free(b)
                for hh in range(4):
                    hd = 4 * h4 + hh
                    i_n = slot_next()
                    P.dma("pool", cWS[i_n], WS[:, i_n, :].rearrange("p (k c) -> p k c", k=8)[:, 0:3, 0:64],
                          wuq[:, hd * 96:hd * 96 + 64].rearrange("(k p) c -> p k c", p=128), writes=[bWS[i_n]])
                    i_kv = slot_next()
                    P.dma("pool", cWS[i_kv], WS[:, i_kv, :].rearrange("p (k c) -> p k c", k=8)[:, 0:2, :],
                          wukv[:, hd * 128:(hd + 1) * 128].rearrange("(k p) c -> p k c", p=128), writes=[bWS[i_kv]])
                    for g in qgroups:
                        n = GS[g]
                        b = ps_alloc()
                        mm(b, [(wk(i_n, k, 0, 64), cq_alias(g, k)) for k in range(3)], [bWS[i_n], bHB[g]], n, 0, 64)
                        P.op("act", lambda h, b=b, g=g, n=n: h.copy(out=QH[0:64, GO[g]:GO[g] + n], in_=PS[b][0:64, 0:n]),
                             reads=[bPS[b]], writes=[bQH])
                        ps_free(b)
                        P.op("dve", lambda h, g=g, n=n, hh=hh: h.tensor_copy(
                            out=QH[64:96, GO[g]:GO[g] + n], in_=QPE[hh * 32:(hh + 1) * 32, GO[g]:GO[g] + n]),
                            reads=[bQPE], writes=[bQH])
                    for g in range(5):
                        n = GS[g]
                        b = ps_alloc()
                        mm(b, [(wk(i_kv, k, 0, 64), ckv_alias(g, k)) for k in range(2)], [bWS[i_kv], bHB[g]], n, 0, 64)
                        P.op("act", lambda h, b=b, g=g, n=n: h.copy(out=KH[0:64, GO[g]:GO[g] + n], in_=PS[b][0:64, 0:n]),
                             reads=[bPS[b]], writes=[bKH])
                        ps_free(b)
                        P.op("dve", lambda h, g=g, n=n: h.tensor_copy(out=KH[64:96, GO[g]:GO[g] + n], in_=kpe_alias(g)),
                             reads=[bHB[g]], writes=[bKH])
                    vcols = (0, 64) if hd % 2 == 0 else (64, 128)
                    ocols = (64, 128) if hd % 2 == 0 else (0, 64)
                    P.op("dve", lambda h, ocols=ocols: h.tensor_copy(
                        out=VH[:, :, ocols[0]:ocols[1]], in_=ONEB[:, :].unsqueeze(1).to_broadcast([128, 18, 64])),
                        reads=[bONEB], writes=[bVH])
                    for t0 in range(0, 18, 8):
                        tn = min(8, 18 - t0)
                        b = ps_alloc()
                        for ti in range(tn):
                            t = t0 + ti
                            g = min(t // 4, 4)
                            tl = t - 4 * g
                            mm(b, [(ckv_alias(g, k, tl * 128, (tl + 1) * 128), wk(i_kv, k, 64, 128)) for k in range(2)],
                               [bWS[i_kv], bHB[g]], 64, c0=ti * 64)
                        P.op("act", lambda h, b=b, t0=t0, tn=tn, vcols=vcols: h.copy(
                            out=VH[:, t0:t0 + tn, vcols[0]:vcols[1]],
                            in_=PS[b][:, 0:tn * 64].rearrange("p (t c) -> p t c", c=64)),
                            reads=[bPS[b]], writes=[bVH])
                        ps_free(b)
                    emit_attn_head(lambda g: QH[0:96, GO[g]:GO[g] + GS[g]], lambda g: [bQH],
                                   lambda t: KH[0:96, t * 128:(t + 1) * 128], lambda t: VH[:, t, :],
                                   lambda t: [bKH, bVH], 96.0 ** -0.5, hd // 2, (hd % 2) * 64, qgroups)

        def emit_store():
            for t in range(16):
                g = t // 4
                par = t % 2
                xo = HT[:, par * 4096:par * 4096 + 2048].bitcast(F32)
                for half in range(2):
                    b = ps_alloc()

                    def fn(h, b=b, half=half, t=t):
                        ins = None
                        for q in range(4):
                            k = half * 4 + q
                            ins = h.matmul(PS[b][:, q * 128:(q + 1) * 128], lhsT=XT[:, k, t * 128:(t + 1) * 128], rhs=IDENT,
                                           start=True, stop=True)
                        return ins
                    P.op("pe", fn, reads=[bCONST] + [bXT[k][g] for k in range(half * 4, half * 4 + 4)],
                         writes=[bPS[b]])
                    P.op("dve", lambda h, b=b, half=half, xo=xo: h.tensor_copy(out=xo[:, half * 512:(half + 1) * 512],
                                                                              in_=PS[b][:, :]),
                         reads=[bPS[b]], writes=[bHB[par]])
                    ps_free(b)
                P.dma("sp", cOUT[par], out_d[t * 128:(t + 1) * 128, :], xo, reads=[bHB[par]])

        emit_load()
        if STAGE >= 0.7:
            emit_mod(0, 0, 48)
        if STAGE >= 1:
            for g in range(5):
                emit_h(0, g, 0, 1)
        for l in range(nlayers if STAGE > 1 else 0):
            need_ctx = l < DEPTH - 1
            groups = list(range(5)) if need_ctx else list(range(4))
            if l % 2 == 0:
                emit_gqa(l, l // 2, need_ctx)
                if STAGE < 3:
                    break
                emit_oproj(l, wog_d[l // 2], groups, True)
            else:
                emit_mla(l, l // 2, need_ctx)
                emit_oproj(l, wom_d[l // 2], groups, False)
            for g in groups:
                emit_ln(l, 0, g)
                emit_h(l, g, 3, 4)
            if STAGE < 4:
                break
            nxt = l + 1 < nlayers
            emit_mlp(l, groups, (lambda e, l=l: emit_mod(l + 1, 6 * e, 6 * e + 6)) if nxt else None)
            for g in groups:
                emit_ln(l, 1, g)
                if nxt:
                    emit_h(l + 1, g, 0, 1)
        emit_store()
        P.final_wait("sp", cOUT)
        P.emit()
    return nc


def _cols(v):
    v = np.asarray(v, np.float32)
    n = v.shape[-1] // 128
    return np.ascontiguousarray(np.moveaxis(v.reshape(v.shape[:-1] + (n, 128)), -1, 0))


_NC_CACHE = {}


def _consts():
    ident = np.eye(128, dtype=np.float32)
    ones = np.ones((128, 128), np.float32)
    bones = np.zeros((128, 128), np.float32)
    bones[0:64, 0:64] = 1.0
    bones[64:128, 64:128] = 1.0
    cm = np.stack([ident, ones, bones, _perm_matrix(64, 2), _perm_matrix(32, 4)], axis=1)
    return (np.ascontiguousarray(cm.reshape(128, 640)), _rope_factors(64, 2), _rope_factors(32, 4))


def make_in_maps(inputs):
    f = lambda a: np.ascontiguousarray(np.asarray(a, np.float32))
    cm, rg, rm = _consts()
    sw = _swap_index(64)
    p64 = np.arange(128) % 64
    gn = np.zeros((128, 2, 4), np.float32)
    for j in range(2):
        qn = np.asarray(inputs["gqa_q_norm"][j], np.float32)
        kn = np.asarray(inputs["gqa_k_norm"][j], np.float32)
        gn[:, j, 0] = qn[p64]
        gn[:, j, 1] = qn[sw[p64]]
        gn[:, j, 2] = kn[p64]
        gn[:, j, 3] = kn[sw[p64]]
    mn = np.concatenate([_cols(inputs["mla_q_norm"]), _cols(inputs["mla_kv_norm"])], axis=2)
    shared = {
        "w_ada": f(inputs["w_ada"]), "bada": f(_cols(inputs["b_ada"]).reshape(128, -1)),
        "lng": f(_cols(inputs["ln_g"]).reshape(128, -1)), "lnb": f(_cols(inputs["ln_b"]).reshape(128, -1)),
        "w1": f(inputs["mlp_w1"]), "w2": f(inputs["mlp_w2"]),
        "wqkv": f(inputs["gqa_w_qkv"]), "wo_g": f(inputs["gqa_w_o"]), "gn": f(gn.reshape(128, 8)),
        "win": f(inputs["mla_w_in"]), "mn": f(mn.reshape(128, 10)), "wuq": f(inputs["mla_w_uq"]),
        "wukv": f(inputs["mla_w_ukv"]), "wo_m": f(inputs["mla_w_o"]),
        "cmats": cm, "ropeg": rg, "ropem": rm,
    }
    cctx = _cols(inputs["c_ctx"])
    maps = []
    for b in range(8):
        cb = _cols(inputs["c"][b])
        cc = np.stack([cb, cctx], axis=2).reshape(128, 16)
        m = dict(shared)
        m["x"] = f(inputs["x"][b])
        m["ctx"] = f(inputs["ctx"][b])
        m["cc"] = f(cc)
        maps.append(m)
    return maps


def kernel(**inputs):
    if "nc" not in _NC_CACHE:
        _NC_CACHE["nc"] = build(DEPTH)
    nc = _NC_CACHE["nc"]
    maps = make_in_maps(inputs)
    res = run_bass_kernel_spmd(nc, maps, core_ids=list(range(8)))
    return np.stack([np.asarray(res.results[b]["out"], np.float32) for b in range(8)], axis=0)
```
